# Optimizing a Trainium2 kernel written in Bass

```python
import math
import jax, jax.numpy as jnp
from jax import lax
import numpy as np

D_MODEL = 1024
BATCH = 4
SEQ = 8192
DEPTH = 2

HEAD_DIM = 64
N_Q_HEADS = 8
N_KV_HEADS = 2
GQA_GROUP = N_Q_HEADS // N_KV_HEADS
ATTN_WIDTH = N_Q_HEADS * HEAD_DIM
KV_WIDTH = N_KV_HEADS * HEAD_DIM
N_CONV_GROUPS = 8
CONV_WIDTH = N_CONV_GROUPS * HEAD_DIM
MIX_WIDTH = ATTN_WIDTH + CONV_WIDTH
CONV_K = 3
WINDOW = 128
BLOCK = 128
D_FF = 2816
NORM_EPS = 1e-6

Q_END = ATTN_WIDTH
K_END = Q_END + KV_WIDTH
V_END = K_END + KV_WIDTH
CB_END = V_END + CONV_WIDTH
CC_END = CB_END + CONV_WIDTH
CH_END = CC_END + CONV_WIDTH
IN_WIDTH = CH_END

kernel_name = "hybrid_macaron_swa_shortconv"


def rms_norm(x, gain):
    xf = x.astype(jnp.float32)
    inv = lax.rsqrt(jnp.mean(xf * xf, axis=-1, keepdims=True) + NORM_EPS)
    return (xf * inv).astype(x.dtype) * gain


def swiglu(h, w_gate, w_up, w_down):
    return (jax.nn.silu(h @ w_gate) * (h @ w_up)) @ w_down


def alibi_slopes(n_heads):
    return jnp.exp2(-8.0 * jnp.arange(1, n_heads + 1, dtype=jnp.float32) / n_heads)


def short_conv(b_gate, c_gate, h_conv, conv_w):
    u = c_gate * h_conv
    w = conv_w.astype(u.dtype)[:, None, :]
    y = lax.conv_general_dilated(
        u, w, window_strides=(1,), padding=[(CONV_K - 1, 0)],
        dimension_numbers=("NWC", "WIO", "NWC"), feature_group_count=CONV_WIDTH)
    return b_gate * y


def sliding_window_attention(q, k, v, sink):
    bsz, seq = q.shape[0], q.shape[1]
    nb = seq // BLOCK
    qb = q.reshape(bsz, nb, BLOCK, N_KV_HEADS, GQA_GROUP, HEAD_DIM)
    kb = k.reshape(bsz, nb, BLOCK, N_KV_HEADS, HEAD_DIM)
    vb = v.reshape(bsz, nb, BLOCK, N_KV_HEADS, HEAD_DIM)
    pad = ((0, 0), (1, 0), (0, 0), (0, 0), (0, 0))
    k_band = jnp.concatenate([jnp.pad(kb, pad)[:, :-1], kb], axis=2)
    v_band = jnp.concatenate([jnp.pad(vb, pad)[:, :-1], vb], axis=2)

    scale = 1.0 / math.sqrt(HEAD_DIM)
    scores = jnp.einsum("bnqkgd,bnskd->bnkgqs", qb, k_band).astype(jnp.float32) * scale

    qi = jnp.arange(BLOCK)[:, None]
    kj = jnp.arange(2 * BLOCK)[None, :]
    dist = qi + BLOCK - kj
    in_window = (dist >= 0) & (dist < WINDOW)
    blk = jnp.arange(nb)[:, None, None]
    mask = in_window[None] & ((blk > 0) | (kj[None] >= BLOCK))

    slopes = alibi_slopes(N_Q_HEADS).reshape(N_KV_HEADS, GQA_GROUP)
    bias = -slopes[:, :, None, None] * dist.astype(jnp.float32)[None, None]
    scores = scores + bias[None, None]
    mask_b = mask[None, :, None, None]
    scores = jnp.where(mask_b, scores, -jnp.inf)

    sink_l = sink.astype(jnp.float32).reshape(1, 1, N_KV_HEADS, GQA_GROUP, 1, 1)
    m = jnp.maximum(jnp.max(scores, axis=-1, keepdims=True), sink_l)
    p = jnp.where(mask_b, jnp.exp(scores - m), 0.0)
    denom = jnp.sum(p, axis=-1, keepdims=True) + jnp.exp(sink_l - m)
    probs = (p / denom).astype(v.dtype)
    out = jnp.einsum("bnkgqs,bnskd->bnqkgd", probs, v_band)
    return out.reshape(bsz, seq, ATTN_WIDTH)


def setup_inputs(seed: int = 0) -> dict:
    key = jax.random.key(seed)
    ks = jax.random.split(key, 20)
    f32 = jnp.float32

    def w(k, shape, fan_in):
        return jax.random.normal(k, shape, f32) * (fan_in ** -0.5)

    def gain(k, shape):
        return 1.0 + 0.05 * jax.random.normal(k, shape, f32)

    return {
        "x": jax.random.normal(ks[0], (BATCH, SEQ, D_MODEL), f32),
        "ffn1_norm": gain(ks[1], (DEPTH, D_MODEL)),
        "ffn1_wg": w(ks[2], (DEPTH, D_MODEL, D_FF), D_MODEL),
        "ffn1_wu": w(ks[3], (DEPTH, D_MODEL, D_FF), D_MODEL),
        "ffn1_wd": w(ks[4], (DEPTH, D_FF, D_MODEL), D_FF),
        "mix_norm": gain(ks[5], (DEPTH, D_MODEL)),
        "w_in": w(ks[6], (DEPTH, D_MODEL, IN_WIDTH), D_MODEL),
        "conv_w": w(ks[7], (DEPTH, CONV_K, CONV_WIDTH), CONV_K),
        "attn_sink": 0.5 * jax.random.normal(ks[8], (DEPTH, N_Q_HEADS), f32),
        "w_out": w(ks[9], (DEPTH, MIX_WIDTH, D_MODEL), MIX_WIDTH),
        "ffn2_norm": gain(ks[10], (DEPTH, D_MODEL)),
        "ffn2_wg": w(ks[11], (DEPTH, D_MODEL, D_FF), D_MODEL),
        "ffn2_wu": w(ks[12], (DEPTH, D_MODEL, D_FF), D_MODEL),
        "ffn2_wd": w(ks[13], (DEPTH, D_FF, D_MODEL), D_FF),
        "final_norm": gain(ks[14], (D_MODEL,)),
    }


def reference(x, ffn1_norm, ffn1_wg, ffn1_wu, ffn1_wd, mix_norm, w_in, conv_w,
              attn_sink, w_out, ffn2_norm, ffn2_wg, ffn2_wu, ffn2_wd, final_norm):
    bsz, seq, _ = x.shape
    for l in range(DEPTH):
        x = x + 0.5 * swiglu(rms_norm(x, ffn1_norm[l]), ffn1_wg[l], ffn1_wu[l], ffn1_wd[l])

        h = rms_norm(x, mix_norm[l])
        z = h @ w_in[l]
        q = z[..., :Q_END].reshape(bsz, seq, N_Q_HEADS, HEAD_DIM)
        k = z[..., Q_END:K_END].reshape(bsz, seq, N_KV_HEADS, HEAD_DIM)
        v = z[..., K_END:V_END].reshape(bsz, seq, N_KV_HEADS, HEAD_DIM)
        attn_out = sliding_window_attention(q, k, v, attn_sink[l])
        conv_out = short_conv(z[..., V_END:CB_END], z[..., CB_END:CC_END],
                              z[..., CC_END:CH_END], conv_w[l])
        x = x + jnp.concatenate([attn_out, conv_out], axis=-1) @ w_out[l]

        x = x + 0.5 * swiglu(rms_norm(x, ffn2_norm[l]), ffn2_wg[l], ffn2_wu[l], ffn2_wd[l])
    return rms_norm(x, final_norm)
```

```python
import numpy as np
import concourse.bass as bass
import concourse.mybir as mybir
from concourse.bass_utils import run_bass_kernel_spmd

F32 = mybir.dt.float32
BF16 = mybir.dt.bfloat16
AF = mybir.ActivationFunctionType
ALU = mybir.AluOpType

NQ, NKV, HD = 8, 2, 64
EPS = 1e-6
HALO = 2


class Cfg:
    def __init__(self, D=1024, F=2816, L=2, chunks=(9, 8, 8, 8), NA=4, NB=3, pause=True):
        self.D, self.F, self.L = D, F, L
        self.DC, self.FC = D // 128, F // 128
        self.chunks = list(chunks)
        self.NBLK = sum(chunks)
        self.NTOK = self.NBLK * 128
        self.NOWN = (self.NBLK - HALO) * 128
        self.CMAXB = max(chunks)
        self.CMAX = self.CMAXB * 128
        self.NA, self.NB = NA, NB
        self.AS = max(self.FC, 22)
        self.NG = 3 * L + 1
        self.pause = pause
        self.pause_heads = 4


def tiles_of(nb):
    if nb <= 4:
        return [nb]
    n = -(-nb // 4)
    base, rem = nb // n, nb % n
    return [base + (1 if i < rem else 0) for i in range(n)]


class Tracker:
    ENGS = ("pe", "act", "dve", "pool", "sp")

    def __init__(self):
        self.ops = []
        self.eng_ops = {e: [] for e in self.ENGS}
        self.last_w = {}
        self.readers = {}
        self.dma_cnt = {}

    def add(self, eng, fn, reads=(), writes=(), dma=None):
        oid = len(self.ops)
        raw, other = set(), set()
        for k in reads:
            w = self.last_w.get(k)
            if w is not None:
                raw.add(w)
        for k in writes:
            w = self.last_w.get(k)
            if w is not None:
                other.add(w)
            rd = self.readers.get(k)
            if rd:
                other.update(rd.values())
        op = dict(id=oid, eng=eng, fn=fn, raw=raw, deps=raw | other, dma=dma,
                  seq=len(self.eng_ops[eng]), inc=False, val=None)
        if dma is not None:
            self.dma_cnt[dma] = self.dma_cnt.get(dma, 0) + 16
            op["val"] = self.dma_cnt[dma]
        self.ops.append(op)
        self.eng_ops[eng].append(op)
        for k in writes:
            self.last_w[k] = oid
            self.readers[k] = {}
        rk = eng if dma is None else ("dma", oid)
        for k in reads:
            self.readers.setdefault(k, {})[rk] = oid
        return oid

    def finalize(self):
        for op in self.ops:
            waits = []
            for d in op["deps"]:
                p = self.ops[d]
                if p["dma"] is not None:
                    waits.append(p)
                    continue
                if p["eng"] != op["eng"]:
                    p["inc"] = True
                    waits.append(p)
                else:
                    if op["dma"] is not None:
                        p["inc"] = True
                        waits.append(p)
                    elif op["eng"] in ("act", "dve", "pool") and d in op["raw"] and op["seq"] - p["seq"] <= 2:
                        p["inc"] = True
                        waits.append(p)
            op["waits"] = waits
        cnt = {e: 0 for e in self.ENGS}
        for op in self.ops:
            if op["dma"] is None and op["inc"]:
                cnt[op["eng"]] += 1
                op["val"] = cnt[op["eng"]]

    def emit(self, eng, engobj, sems, dma_sems):
        known = {}
        n_wait = 0
        for op in self.eng_ops[eng]:
            need = {}
            for p in op["waits"]:
                if p["dma"] is not None:
                    s = ("dma", p["dma"])
                else:
                    s = ("eng", p["eng"])
                if p["val"] > need.get(s, 0):
                    need[s] = p["val"]
            for s, v in need.items():
                if known.get(s, 0) >= v:
                    continue
                known[s] = v
                sem = dma_sems[s[1]] if s[0] == "dma" else sems[s[1]]
                engobj.wait_ge(sem, v)
                n_wait += 1
            ins = op["fn"](engobj)
            if op["dma"] is not None:
                ins.then_inc(dma_sems[op["dma"]], 16)
            elif op["inc"]:
                ins.then_inc(sems[eng], 1)
        return n_wait


def build_program(cfg):
    D, F, L, DC, FC = cfg.D, cfg.F, cfg.L, cfg.DC, cfg.FC
    NA, NB, AS, CMAX, CMAXB = cfg.NA, cfg.NB, cfg.AS, cfg.CMAX, cfg.CMAXB
    nc = bass.Bass("TRN2", target_bir_lowering=False)
    T = Tracker()

    d_x = nc.dram_tensor("xT", [D, cfg.NTOK], F32, kind="ExternalInput").ap()
    d_y = nc.dram_tensor("yT", [D, cfg.NTOK], F32, kind="ExternalOutput").ap()
    d_wgu = nc.dram_tensor("wgu", [L * 2 * FC, 128, 2 * DC * 128], F32, kind="ExternalInput").ap()
    d_wd = nc.dram_tensor("wd", [L * 2 * DC, 128, FC * 128], F32, kind="ExternalInput").ap()
    d_win = nc.dram_tensor("win", [L * 9, 128, 2 * DC * 128], F32, kind="ExternalInput").ap()
    d_wout = nc.dram_tensor("wout", [L * (DC // 2), 128, 2 * 8 * 128], F32, kind="ExternalInput").ap()
    d_gain = nc.dram_tensor("gains", [128, cfg.NG * DC], F32, kind="ExternalInput").ap()
    d_convw = nc.dram_tensor("convw", [128, L * 12], F32, kind="ExternalInput").ap()
    d_sink = nc.dram_tensor("sink", [1, L * 8], F32, kind="ExternalInput").ap()
    d_masks = nc.dram_tensor("masks", [128, 3 * 128], F32, kind="ExternalInput").ap()
    d_qaug = nc.dram_tensor("qaug", [3, 8 * 128], F32, kind="ExternalInput").ap()
    d_kaug = nc.dram_tensor("kaug", [3, 128], F32, kind="ExternalInput").ap()

    import contextlib
    es = contextlib.ExitStack()
    with es:
        def sb(name, shape, dt):
            return es.enter_context(nc.sbuf_tensor(name, shape, dt))

        xs = sb("xs", [128, DC, CMAX], F32)
        hT = sb("hT", [128, DC, CMAX], BF16)
        aT = sb("aT", [128, AS, CMAX], BF16)
        wA = [sb(f"wA{i}", [128, 2, 8, 128], BF16) for i in range(NA)]
        wB = [sb(f"wB{i}", [128, FC, 128], BF16) for i in range(NB)]
        sgt = [sb(f"sg{i}", [128, 512], F32) for i in range(2)]
        sqt = [sb(f"sq{i}", [128, 512], BF16) for i in range(2 * DC)]
        rstd = [sb(f"rstd{i}", [128, 512], F32) for i in range(2)]
        ones_bf = sb("ones_bf", [128, 128], BF16)
        gains = sb("gains_sb", [128, cfg.NG * DC], F32)
        convw = sb("convw_sb", [128, L * 12], F32)
        sink_sb = sb("sink_sb", [1, L * 8], F32)
        sinkexp = sb("sinkexp", [1, L * 8], F32)
        sinkrow = sb("sinkrow", [1, L * 8, 128], BF16)
        sel = sb("sel", [1, 2, 128], BF16)
        masks = sb("masks_sb", [128, 3, 128], BF16)
        qaug_c = sb("qaug_c", [128, 8, 128], BF16)
        kaug_c = sb("kaug_c", [128, 128], BF16)
        KTc = [sb(f"KTc{l}", [128, 2, 128], BF16) for l in range(L)]
        Vc = [sb(f"Vc{l}", [128, 2, 192], BF16) for l in range(L)]
        Vb = sb("Vb", [128, CMAXB, 2, 192], BF16)
        ub = sb("ub", [128, 2 + CMAX], F32)
        ucarry = sb("ucarry", [128, L * 4, 2], F32)
        PT = [sb(f"PT{i}", [128, 512], BF16) for i in range(6)]
        rscr = [sb(f"rscr{i}", [128, 512], F32) for i in range(2)]
        den = [sb(f"den{i}", [128, 512], F32) for i in range(2)]
        rec = [sb(f"rec{i}", [128, 512], F32) for i in range(2)]
        ps = [es.enter_context(nc.psum_tensor(f"ps{i}", [128, 512], F32)) for i in range(8)]

        tC_full = aT[:, 18:20, :].rearrange("p a c -> p (a c)").bitcast(F32)
        yb_full = aT[:, 20:22, :].rearrange("p a c -> p (a c)").bitcast(F32)

        sems = {e: es.enter_context(nc.semaphore(f"s_{e}")) for e in Tracker.ENGS}
        dma_keys = ([("wA", i) for i in range(NA)] + [("wB", i) for i in range(NB)] +
                    [("xld", i) for i in range(4)] + [("st", i) for i in range(4)] + [("cst", i) for i in range(6)])
        dma_sems = {k: es.enter_context(nc.semaphore("d_%s%d" % k)) for k in dma_keys}

        ring_ctr = {}

        def ring(name, n):
            i = ring_ctr.get(name, 0)
            ring_ctr[name] = i + 1
            return i % n

        wlistA, wlistB = [], []
        issuedA, issuedB = [0], [0]

        def plan_weights():
            for _ci in range(len(cfg.chunks)):
                for l in range(L):
                    for n in range(2):
                        if n == 1:
                            for p_ in range(9):
                                wlistA.append((d_win[l * 9 + p_], DC))
                            for p_ in range(DC // 2):
                                wlistA.append((d_wout[l * (DC // 2) + p_], 8))
                        for fc in range(FC):
                            wlistA.append((d_wgu[(l * 2 + n) * FC + fc], DC))
                        for dc in range(DC):
                            wlistB.append(d_wd[(l * 2 + n) * DC + dc])
        plan_weights()
        useA, useB = [0], [0]

        def issue_A(upto):
            while issuedA[0] <= upto and issuedA[0] < len(wlistA):
                i = issuedA[0]
                issuedA[0] += 1
                src, kc = wlistA[i]
                slot = i % NA
                dst = wA[slot][:, :, 0:kc, :]
                srcv = src.rearrange("p (s c f) -> p s c f", s=2, c=kc)
                T.add("pool", lambda e, dst=dst, srcv=srcv: e.dma_start(out=dst, in_=srcv),
                      writes=[("wA", slot)], dma=("wA", slot))

        def issue_B(upto):
            while issuedB[0] <= upto and issuedB[0] < len(wlistB):
                i = issuedB[0]
                issuedB[0] += 1
                src = wlistB[i]
                slot = i % NB
                dst = wB[slot][:, :, :]
                srcv = src.rearrange("p (k f) -> p k f", k=FC)
                T.add("pool", lambda e, dst=dst, srcv=srcv: e.dma_start(out=dst, in_=srcv),
                      writes=[("wB", slot)], dma=("wB", slot))

        def next_A(hold=0):
            i = useA[0]
            useA[0] += 1
            issue_A(i + NA - 1 - hold)
            issue_B(useB[0] + NB - 2)
            return i % NA

        def next_B():
            i = useB[0]
            useB[0] += 1
            issue_B(i + NB - 1)
            issue_A(useA[0] + NA - 2)
            return i % NB

        T.add("sp", lambda e: e.dma_start(out=gains[:, :], in_=d_gain[:, :]), writes=[("gains",)], dma=("cst", 0))
        T.add("sp", lambda e: e.dma_start(out=convw[:, :], in_=d_convw[:, :]), writes=[("convw",)], dma=("cst", 1))
        T.add("sp", lambda e: e.dma_start(out=sink_sb[:, :], in_=d_sink[:, :]), writes=[("sink",)], dma=("cst", 2))
        T.add("pool", lambda e: e.dma_start(out=masks[:, :, :], in_=d_masks.rearrange("p (m k) -> p m k", m=3)),
              writes=[("masks",)], dma=("cst", 3))
        T.add("pool", lambda e: e.dma_start(out=qaug_c[64:67, :, :], in_=d_qaug.rearrange("p (h k) -> p h k", h=8)),
              writes=[("qaug",)], dma=("cst", 4))
        T.add("pool", lambda e: e.dma_start(out=kaug_c[64:67, :], in_=d_kaug[:, :]),
              writes=[("kaug",)], dma=("cst", 5))
        T.add("dve", lambda e: e.memset(ones_bf[:, :], 1.0), writes=[("ones",)])
        T.add("dve", lambda e: e.memset(sel[:, :, :], 1.0), writes=[("sel",)])
        T.add("dve", lambda e: e.memset(sel[0:1, 0, 0:64], 0.0), writes=[("sel",)])
        T.add("dve", lambda e: e.memset(sel[0:1, 1, 64:128], 0.0), writes=[("sel",)])
        T.add("dve", lambda e: e.memset(ucarry[:, :, :], 0.0), writes=[("ucarry", i) for i in range(L * 4)])
        for l in range(L):
            T.add("dve", lambda e, l=l: e.memset(KTc[l][:, :, :], 0.0), writes=[("KTc", l)])
            T.add("dve", lambda e, l=l: e.memset(Vc[l][:, :, :], 1.0), writes=[("Vc", l)])
            T.add("dve", lambda e, l=l: e.memset(Vc[l][:, :, 64:128], 0.0), writes=[("Vc", l)])
            T.add("pool", lambda e, l=l: e.tensor_copy(
                out=KTc[l][64:67, :, :], in_=kaug_c[64:67, :].unsqueeze(1).broadcast_to([3, 2, 128])),
                reads=[("kaug",)], writes=[("KTc", l)])
        T.add("dve", lambda e: e.memset(Vb[:, :, :, :], 1.0), writes=[("Vb", b) for b in range(CMAXB)])
        T.add("act", lambda e: e.activation(out=sinkexp[:, :], in_=sink_sb[:, :], func=AF.Exp),
              reads=[("sink",)], writes=[("sinkexp",)])
        T.add("dve", lambda e: e.tensor_copy(
            out=sinkrow[:, :, :], in_=sinkexp[:, :].unsqueeze(2).broadcast_to([1, L * 8, 128])),
            reads=[("sinkexp",)], writes=[("sinkrow",)])

        def kx(c, tl):
            return [("x", c, b) for b in range(tl[0], tl[0] + tl[1])]

        def kh(tl):
            return [("h", b) for b in range(tl[0], tl[0] + tl[1])]

        def ka(slot, tl):
            return [("a", slot, b) for b in range(tl[0], tl[0] + tl[1])]

        def tok(tl):
            return slice(tl[0] * 128, (tl[0] + tl[1]) * 128)

        def recip(out_ap, in_ap, scr_ap, reads, wkey, skey, scale=-1.0, in_scale=1.0, in_bias=0.0):
            T.add("act", lambda e: e.activation(out=scr_ap, in_=in_ap, func=AF.Ln, bias=in_bias, scale=in_scale),
                  reads=reads, writes=[skey])
            T.add("act", lambda e: e.activation(out=out_ap, in_=scr_ap, func=AF.Exp, scale=scale),
                  reads=[skey], writes=[wkey])

        NSQ = 2 * DC
        deferred = []

        def flush_deferred():
            while deferred:
                deferred.pop(0)()

        def norm_A(tl):
            N = tl[1] * 128
            ts = tok(tl)
            slots = []
            for c in range(DC):
                r = ring("sq", NSQ)
                slots.append(r)
                T.add("act", lambda e, r=r, c=c, ts=ts, N=N: e.activation(
                    out=sqt[r][:, 0:N], in_=xs[:, c, ts], func=AF.Square),
                    reads=kx(c, tl), writes=[("sq", r)])
            return slots

        def norm_B(tl, slots, gi, final):
            N = tl[1] * 128
            ts = tok(tl)
            pb = 6 + ring("psn", 2)
            for c in range(DC):
                r = slots[c]
                T.add("pe", lambda e, r=r, c=c, N=N, pb=pb: e.matmul(
                    ps[pb][:, 0:N], ones_bf[:, :], sqt[r][:, 0:N], start=(c == 0), stop=(c == DC - 1)),
                    reads=[("sq", r), ("ones",)], writes=[("ps", pb)])
            rr = ring("rstd", 2)
            recip(rstd[rr][:, 0:N], ps[pb][:, 0:N], rscr[rr][:, 0:N], [("ps", pb)],
                  ("rstd", rr), ("rscr", rr), scale=-0.5, in_scale=1.0 / D, in_bias=EPS)
            for c in range(DC):
                gcol = gains[:, gi * DC + c: gi * DC + c + 1]
                if final:
                    T.add("dve", lambda e, c=c, ts=ts, N=N, rr=rr, gcol=gcol: e.scalar_tensor_tensor(
                        out=xs[:, c, ts], in0=xs[:, c, ts], scalar=gcol, in1=rstd[rr][:, 0:N],
                        op0=ALU.mult, op1=ALU.mult),
                        reads=kx(c, tl) + [("rstd", rr), ("gains",)], writes=kx(c, tl))
                else:
                    T.add("dve", lambda e, c=c, ts=ts, N=N, rr=rr, gcol=gcol: e.scalar_tensor_tensor(
                        out=hT[:, c, ts], in0=xs[:, c, ts], scalar=gcol, in1=rstd[rr][:, 0:N],
                        op0=ALU.mult, op1=ALU.mult),
                        reads=kx(c, tl) + [("rstd", rr), ("gains",)], writes=kh(tl))

        def norm_plain(tiles, gi, final=False):
            for tl in tiles:
                norm_B(tl, norm_A(tl), gi, final)

        saved_sq = {}

        def last_stage_hook(i, tiles, nxt):
            gi, final = nxt
            if i > 0:
                norm_B(tiles[i - 1], saved_sq[i - 1], gi, final)
            saved_sq[i] = norm_A(tiles[i])
            if i == len(tiles) - 1:
                tl_, sl_ = tiles[i], saved_sq[i]
                if final or len(tiles) == 1:
                    norm_B(tl_, sl_, gi, final)
                else:
                    deferred.append(lambda: norm_B(tl_, sl_, gi, final))

        def ffn(tiles, l, n, nxt):
            order = []
            if len(tiles) >= 2 and FC >= 2:
                for t_ in tiles[:-1]:
                    order.append((0, t_))
                for t_ in tiles[:-1]:
                    order.append((1, t_))
                order += [(0, tiles[-1]), (1, tiles[-1])]
                order += [(fc, t_) for fc in range(2, FC) for t_ in tiles]
            else:
                order = [(fc, t_) for fc in range(FC) for t_ in tiles]
            fslots = {}
            for fc, tl in order:
                if fc not in fslots:
                    fslots[fc] = next_A(hold=1 if (fc == 1 and len(tiles) >= 2 and FC >= 2) else 0)
                slot = fslots[fc]
                if True:
                    N = tl[1] * 128
                    ts = tok(tl)
                    gb = 0 + ring("psg", 2)
                    ubk = 2 + ring("psu", 2)
                    for s, bank in ((0, gb), (1, ubk)):
                        for c in range(DC):
                            T.add("pe", lambda e, s=s, c=c, bank=bank, slot=slot, ts=ts, N=N: e.matmul(
                                ps[bank][:, 0:N], wA[slot][:, s, c, :], hT[:, c, ts],
                                start=(c == 0), stop=(c == DC - 1)),
                                reads=[("wA", slot)] + kh(tl), writes=[("ps", bank)])
                    r = ring("sg", 2)
                    T.add("act", lambda e, r=r, gb=gb, N=N: e.activation(
                        out=sgt[r][:, 0:N], in_=ps[gb][:, 0:N], func=AF.Silu),
                        reads=[("ps", gb)], writes=[("sg", r)])
                    T.add("dve", lambda e, r=r, ubk=ubk, N=N, fc=fc, ts=ts: e.tensor_tensor(
                        out=aT[:, fc, ts], in0=ps[ubk][:, 0:N], in1=sgt[r][:, 0:N], op=ALU.mult),
                        reads=[("ps", ubk), ("sg", r)], writes=ka(fc, tl))
                    flush_deferred()
            for dc in range(DC):
                slot = next_B()
                for tl in tiles:
                    N = tl[1] * 128
                    ts = tok(tl)
                    yb_ = 4 + ring("psy", 2)
                    for k in range(FC):
                        T.add("pe", lambda e, k=k, yb_=yb_, slot=slot, ts=ts, N=N: e.matmul(
                            ps[yb_][:, 0:N], wB[slot][:, k, :], aT[:, k, ts],
                            start=(k == 0), stop=(k == FC - 1)),
                            reads=[("wB", slot)] + ka(k, tl), writes=[("ps", yb_)])
                    T.add("dve", lambda e, dc=dc, yb_=yb_, ts=ts, N=N: e.scalar_tensor_tensor(
                        out=xs[:, dc, ts], in0=ps[yb_][:, 0:N], scalar=0.5, in1=xs[:, dc, ts],
                        op0=ALU.mult, op1=ALU.add),
                        reads=[("ps", yb_)] + kx(dc, tl), writes=kx(dc, tl))
                    if dc == DC - 1:
                        last_stage_hook(tiles.index(tl), tiles, nxt)

        def mix(ci, nb, tiles, l, nxt):
            C = nb * 128
            allt = (0, nb)
            flush_deferred()
            PH = cfg.pause_heads
            if cfg.pause:
                T.add("pool", lambda e: e.tensor_copy(
                    out=aT[64:67, 0:PH, 0:C].rearrange("p h (b t) -> p h b t", t=128),
                    in_=qaug_c[64:67, 0:PH, :].unsqueeze(2).broadcast_to([3, PH, nb, 128])),
                    reads=[("qaug",)] + kx(DC - 1, tiles[-1]), writes=[("pause",)])
            for tl in tiles:
                ts = tok(tl)
                T.add("act", lambda e, ts=ts, tl=tl: e.activation(
                    out=aT[64:67, 16:18, ts].rearrange("p h (b t) -> p h b t", t=128),
                    in_=kaug_c[64:67, :].unsqueeze(1).unsqueeze(1).broadcast_to([3, 2, tl[1], 128]),
                    func=AF.Copy),
                    reads=[("kaug",)], writes=[k for h in (16, 17) for k in ka(h, tl)])
                T.add("act", lambda e, ts=ts, tl=tl: e.activation(
                    out=aT[64:67, 0:8, ts].rearrange("p h (b t) -> p h b t", t=128),
                    in_=qaug_c[64:67, :, :].unsqueeze(2).broadcast_to([3, 8, tl[1], 128]),
                    func=AF.Copy),
                    reads=[("qaug",)], writes=[k for h in range(8) for k in ka(h, tl)])

            unit_names = ["q0", "q1", "q2", "q3", "k", "v"]
            for j in range(4):
                unit_names += [f"C{j}", f"H{j}", f"B{j}"]
            slot = None
            for ui, un in enumerate(unit_names):
                s = ui % 2
                if s == 0:
                    slot = next_A()
                if un == "v":
                    for tl in tiles:
                        bank = ring("psp", 4)
                        for jb in range(tl[1]):
                            b = tl[0] + jb
                            for c in range(DC):
                                T.add("pe", lambda e, s=s, c=c, bank=bank, slot=slot, jb=jb, b=b: e.matmul(
                                    ps[bank][:, jb * 128:(jb + 1) * 128], hT[:, c, b * 128:(b + 1) * 128],
                                    wA[slot][:, s, c, :], start=(c == 0), stop=(c == DC - 1)),
                                    reads=[("wA", slot), ("h", b)], writes=[("ps", bank)])
                        T.add("act", lambda e, bank=bank, tl=tl: e.activation(
                            out=Vb[:, tl[0]:tl[0] + tl[1], :, 64:128],
                            in_=ps[bank][:, 0:tl[1] * 128].rearrange("p (b g d) -> p b g d", g=2, d=64),
                            func=AF.Copy),
                            reads=[("ps", bank)], writes=[("Vb", b) for b in range(tl[0], tl[0] + tl[1])])
                    continue
                for tl in tiles:
                    N = tl[1] * 128
                    ts = tok(tl)
                    bank = ring("psp", 4)
                    for c in range(DC):
                        T.add("pe", lambda e, s=s, c=c, bank=bank, slot=slot, ts=ts, N=N: e.matmul(
                            ps[bank][:, 0:N], wA[slot][:, s, c, :], hT[:, c, ts],
                            start=(c == 0), stop=(c == DC - 1)),
                            reads=[("wA", slot)] + kh(tl) + ([("pause",)] if ui == 0 else []), writes=[("ps", bank)])
                    if un[0] == "q":
                        i = int(un[1])
                        T.add("act", lambda e, bank=bank, i=i, ts=ts, N=N: e.activation(
                            out=aT[0:64, 2 * i, ts], in_=ps[bank][0:64, 0:N], func=AF.Copy),
                            reads=[("ps", bank)], writes=ka(2 * i, tl))
                        T.add("dve", lambda e, bank=bank, i=i, ts=ts, N=N: e.tensor_copy(
                            out=aT[0:64, 2 * i + 1, ts], in_=ps[bank][64:128, 0:N]),
                            reads=[("ps", bank)], writes=ka(2 * i + 1, tl))
                    elif un == "k":
                        T.add("act", lambda e, bank=bank, ts=ts, N=N: e.activation(
                            out=aT[0:64, 16, ts], in_=ps[bank][0:64, 0:N], func=AF.Copy),
                            reads=[("ps", bank)], writes=ka(16, tl))
                        T.add("dve", lambda e, bank=bank, ts=ts, N=N: e.tensor_copy(
                            out=aT[0:64, 17, ts], in_=ps[bank][64:128, 0:N]),
                            reads=[("ps", bank)], writes=ka(17, tl))
                    elif un[0] == "C":
                        T.add("act", lambda e, bank=bank, ts=ts, N=N: e.activation(
                            out=tC_full[:, ts], in_=ps[bank][:, 0:N], func=AF.Copy),
                            reads=[("ps", bank)], writes=ka(18, tl) + ka(19, tl))
                    elif un[0] == "H":
                        j = int(un[1])
                        T.add("dve", lambda e, bank=bank, ts=ts, N=N, tl=tl: e.tensor_tensor(
                            out=ub[:, 2 + tl[0] * 128: 2 + (tl[0] + tl[1]) * 128], in0=ps[bank][:, 0:N],
                            in1=tC_full[:, ts], op=ALU.mult),
                            reads=[("ps", bank)] + ka(18, tl) + ka(19, tl),
                            writes=[("ub", b) for b in range(tl[0], tl[0] + tl[1])])
                        if tl is tiles[-1]:
                            ubk_all = [("ub", b) for b in range(nb)]
                            ykeys = ka(20, allt) + ka(21, allt)
                            wbase = (l * 3) * 4 + j
                            T.add("pool", lambda e, l=l, j=j: e.tensor_copy(
                                out=ub[:, 0:2], in_=ucarry[:, l * 4 + j, :]),
                                reads=[("ucarry", l * 4 + j)], writes=[("ub", "c")])
                            T.add("dve", lambda e, wbase=wbase: e.tensor_scalar(
                                out=yb_full[:, 0:C], in0=ub[:, 2:2 + C], scalar1=convw[:, wbase + 8: wbase + 9],
                                scalar2=None, op0=ALU.mult),
                                reads=ubk_all + [("convw",)], writes=ykeys)
                            T.add("dve", lambda e, wbase=wbase: e.scalar_tensor_tensor(
                                out=yb_full[:, 0:C], in0=ub[:, 1:1 + C], scalar=convw[:, wbase + 4: wbase + 5],
                                in1=yb_full[:, 0:C], op0=ALU.mult, op1=ALU.add),
                                reads=ubk_all + [("ub", "c"), ("convw",)] + ykeys, writes=ykeys)
                            T.add("dve", lambda e, wbase=wbase: e.scalar_tensor_tensor(
                                out=yb_full[:, 0:C], in0=ub[:, 0:C], scalar=convw[:, wbase: wbase + 1],
                                in1=yb_full[:, 0:C], op0=ALU.mult, op1=ALU.add),
                                reads=ubk_all + [("ub", "c"), ("convw",)] + ykeys, writes=ykeys)
                            T.add("pool", lambda e, l=l, j=j: e.tensor_copy(
                                out=ucarry[:, l * 4 + j, :], in_=ub[:, C:C + 2]),
                                reads=ubk_all, writes=[("ucarry", l * 4 + j)])
                    elif un[0] == "B":
                        j = int(un[1])
                        T.add("dve", lambda e, bank=bank, ts=ts, N=N, j=j: e.tensor_tensor(
                            out=aT[:, 12 + j, ts], in0=ps[bank][:, 0:N], in1=yb_full[:, ts], op=ALU.mult),
                            reads=[("ps", bank)] + ka(20, tl) + ka(21, tl), writes=ka(12 + j, tl))
                    flush_deferred()

            items = [(b, g) for b in range(nb) for g in range(2)]
            issue_A(useA[0] + NA - 2)

            def emit_scores(b, g):
                tl1 = (b, 1)
                res = []
                for kb in (0, 1):
                    nr = 67 if kb == 0 else 66
                    if kb == 1:
                        kap = aT[0:nr, 16 + g, b * 128:(b + 1) * 128]
                        kkeys = [("a", 16 + g, b)]
                    elif b == 0:
                        kap = KTc[l][0:nr, g, :]
                        kkeys = [("KTc", l)]
                    else:
                        kap = aT[0:nr, 16 + g, (b - 1) * 128: b * 128]
                        kkeys = [("a", 16 + g, b - 1)]
                    bank = ring("pss", 6)
                    qap = aT[0:nr, 4 * g:4 * g + 4, b * 128:(b + 1) * 128]
                    T.add("pe", lambda e, bank=bank, kap=kap, qap=qap: e.matmul(
                        ps[bank][:, :].rearrange("p (h t) -> p h t", t=128), kap, qap, start=True, stop=True),
                        reads=kkeys + [("a", 4 * g + hh, b) for hh in range(4)], writes=[("ps", bank)])
                    pr = ring("PT", 6)
                    T.add("act", lambda e, bank=bank, pr=pr: e.activation(
                        out=PT[pr][:, :], in_=ps[bank][:, :], func=AF.Exp, scale=0.125),
                        reads=[("ps", bank)], writes=[("PT", pr)])
                    if kb == 1:
                        mi = 0
                    else:
                        mi = 2 if (ci == 0 and b == 0) else 1
                    T.add("pool", lambda e, pr=pr, mi=mi: e.tensor_tensor(
                        out=PT[pr][:, :].rearrange("p (h t) -> p h t", t=128),
                        in0=PT[pr][:, :].rearrange("p (h t) -> p h t", t=128),
                        in1=masks[:, mi, :].unsqueeze(1).broadcast_to([128, 4, 128]), op=ALU.mult),
                        reads=[("PT", pr), ("masks",)], writes=[("PT", pr)])
                    res.append(pr)
                return res

            def emit_pv(b, g, prs):
                ob = 6 + ring("pso", 2)
                vaps, vkeys = [], []
                for kb in (0, 1):
                    if kb == 1:
                        vaps.append(Vb[:, b, g, :]); vkeys.append(("Vb", b))
                    elif b == 0:
                        vaps.append(Vc[l][:, g, :]); vkeys.append(("Vc", l))
                    else:
                        vaps.append(Vb[:, b - 1, g, :]); vkeys.append(("Vb", b - 1))
                for par in (0, 1):
                    osl = slice(par * 256, (par + 1) * 256)
                    for kb in (0, 1):
                        lhs = vaps[kb][:, 64:192] if par == 0 else vaps[kb][:, 0:128]
                        rhs = PT[prs[kb]][:, :].rearrange("p (h t) -> p h t", t=128)[:, par:4:2, :]
                        T.add("pe", lambda e, ob=ob, osl=osl, lhs=lhs, rhs=rhs, kb=kb: e.matmul(
                            ps[ob][:, osl].rearrange("p (h t) -> p h t", t=128), lhs, rhs,
                            start=(kb == 0), stop=False),
                            reads=[vkeys[kb], ("PT", prs[kb])], writes=[("ps", ob)])
                    h0 = l * 8 + 4 * g + par
                    T.add("pe", lambda e, ob=ob, osl=osl, par=par, h0=h0: e.matmul(
                        ps[ob][:, osl].rearrange("p (h t) -> p h t", t=128), sel[0:1, par, :],
                        sinkrow[0:1, h0:h0 + 3:2, :], start=False, stop=True),
                        reads=[("sel",), ("sinkrow",)], writes=[("ps", ob)])
                rr = ring("rec", 2)
                T.add("dve", lambda e, ob=ob, rr=rr: e.tensor_copy(
                    out=den[rr][0:64, 0:256], in_=ps[ob][64:128, 0:256]),
                    reads=[("ps", ob)], writes=[("den", rr, 0)])
                T.add("dve", lambda e, ob=ob, rr=rr: e.tensor_copy(
                    out=den[rr][64:128, 256:512], in_=ps[ob][0:64, 256:512]),
                    reads=[("ps", ob)], writes=[("den", rr, 1)])
                recip(rec[rr][0:64, 0:256], den[rr][0:64, 0:256], rscr[rr][0:64, 0:256],
                      [("den", rr, 0)], ("rec", rr, 0), ("rscr", rr, 0))
                recip(rec[rr][64:128, 256:512], den[rr][64:128, 256:512], rscr[rr][64:128, 256:512],
                      [("den", rr, 1)], ("rec", rr, 1), ("rscr", rr, 1))
                bs = slice(b * 128, (b + 1) * 128)
                T.add("dve", lambda e, ob=ob, rr=rr, bs=bs, g=g: e.tensor_tensor(
                    out=aT[0:64, 8 + 2 * g: 8 + 2 * g + 2, bs],
                    in0=ps[ob][0:64, 0:256].rearrange("p (h t) -> p h t", t=128),
                    in1=rec[rr][0:64, 0:256].rearrange("p (h t) -> p h t", t=128), op=ALU.mult),
                    reads=[("ps", ob), ("rec", rr, 0)],
                    writes=[("a", 8 + 2 * g, b, 0), ("a", 8 + 2 * g + 1, b, 0)])
                T.add("dve", lambda e, ob=ob, rr=rr, bs=bs, g=g: e.tensor_tensor(
                    out=aT[64:128, 8 + 2 * g: 8 + 2 * g + 2, bs],
                    in0=ps[ob][64:128, 256:512].rearrange("p (h t) -> p h t", t=128),
                    in1=rec[rr][64:128, 256:512].rearrange("p (h t) -> p h t", t=128), op=ALU.mult),
                    reads=[("ps", ob), ("rec", rr, 1)],
                    writes=[("a", 8 + 2 * g, b, 1), ("a", 8 + 2 * g + 1, b, 1)])

            pend = [emit_scores(*items[0])]
            if len(items) > 1:
                pend.append(emit_scores(*items[1]))
            for i, (b, g) in enumerate(items):
                if i + 2 < len(items):
                    pend.append(emit_scores(*items[i + 2]))
                emit_pv(b, g, pend.pop(0))
            T.add("act", lambda e: e.activation(
                out=KTc[l][0:64, :, :], in_=aT[0:64, 16:18, (nb - 1) * 128: nb * 128], func=AF.Copy),
                reads=[("a", 16, nb - 1), ("a", 17, nb - 1)], writes=[("KTc", l)])
            T.add("act", lambda e: e.activation(
                out=Vc[l][:, :, 64:128], in_=Vb[:, nb - 1, :, 64:128], func=AF.Copy),
                reads=[("Vb", nb - 1)], writes=[("Vc", l)])

            slot = None
            for dc in range(DC):
                s = dc % 2
                if s == 0:
                    slot = next_A()
                for tl in tiles:
                    N = tl[1] * 128
                    ts = tok(tl)
                    bank = ring("psp", 4)
                    mk = []
                    for c in range(8):
                        if c < 4:
                            mk_c = [("a", 8 + c, b, hf) for b in range(tl[0], tl[0] + tl[1]) for hf in (0, 1)]
                        else:
                            mk_c = ka(8 + c, tl)
                        T.add("pe", lambda e, s=s, c=c, bank=bank, slot=slot, ts=ts, N=N: e.matmul(
                            ps[bank][:, 0:N], wA[slot][:, s, c, :], aT[:, 8 + c, ts],
                            start=(c == 0), stop=(c == 7)),
                            reads=[("wA", slot)] + mk_c, writes=[("ps", bank)])
                    T.add("dve", lambda e, dc=dc, bank=bank, ts=ts, N=N: e.tensor_tensor(
                        out=xs[:, dc, ts], in0=ps[bank][:, 0:N], in1=xs[:, dc, ts], op=ALU.add),
                        reads=[("ps", bank)] + kx(dc, tl), writes=kx(dc, tl))
                    if dc == DC - 1:
                        last_stage_hook(tiles.index(tl), tiles, nxt)

        _ka_orig = ka

        def ka(slot, tl):
            ks = _ka_orig(slot, tl)
            if 8 <= slot < 12:
                ks = ks + [("a", slot, b, hf) for b in range(tl[0], tl[0] + tl[1]) for hf in (0, 1)]
            return ks

        xT_v = d_x.rearrange("(c p) t -> p c t", p=128)
        yT_v = d_y.rearrange("(c p) t -> p c t", p=128)
        tok0 = 0
        for ci, nb in enumerate(cfg.chunks):
            tsz = tiles_of(nb)
            tiles, o = [], 0
            for t_ in tsz:
                tiles.append((o, t_))
                o += t_
            for ti, tl in enumerate(tiles):
                ts = tok(tl)
                g0 = tok0 + tl[0] * 128
                T.add("sp", lambda e, ts=ts, g0=g0, N=tl[1] * 128: e.dma_start(
                    out=xs[:, :, ts], in_=xT_v[:, :, g0:g0 + N]),
                    writes=[k for c in range(DC) for k in kx(c, tl)], dma=("xld", ti))
            norm_plain(tiles, 0)
            for l in range(L):
                ffn(tiles, l, 0, (l * 3 + 1, False))
                mix(ci, nb, tiles, l, (l * 3 + 2, False))
                ffn(tiles, l, 1, ((l + 1) * 3, False) if l + 1 < L else (3 * L, True))
            flush_deferred()
            for ti, tl in enumerate(tiles):
                g0 = tok0 + tl[0] * 128
                g1 = g0 + tl[1] * 128
                ts = tok(tl)
                T.add("sp", lambda e, ts=ts, g0=g0, g1=g1: e.dma_start(
                    out=yT_v[:, :, g0:g1], in_=xs[:, :, ts]),
                    reads=[k for c in range(DC) for k in kx(c, tl)], dma=("st", ti))
            tok0 += nb * 128

        T.finalize()
        final_waits = [(dma_sems[k], v) for k, v in T.dma_cnt.items() if k[0] == "st"]

        with nc.Block() as block:
            @block.tensor
            def _(e):
                T.emit("pe", e, sems, dma_sems)

            @block.scalar
            def _(e):
                T.emit("act", e, sems, dma_sems)

            @block.vector
            def _(e):
                T.emit("dve", e, sems, dma_sems)

            @block.gpsimd
            def _(e):
                T.emit("pool", e, sems, dma_sems)

            @block.sync
            def _(e):
                T.emit("sp", e, sems, dma_sems)
                for s_, v in final_waits:
                    e.wait_ge(s_, v)
    return nc, T


def prep_weights(cfg, ffn1_norm, ffn1_wg, ffn1_wu, ffn1_wd, mix_norm, w_in, conv_w, attn_sink, w_out,
                 ffn2_norm, ffn2_wg, ffn2_wu, ffn2_wd, final_norm):
    D, F, L, DC, FC = cfg.D, cfg.F, cfg.L, cfg.DC, cfg.FC
    f32 = np.float32

    def kxm(w, cols):
        K = w.shape[0]
        return np.ascontiguousarray(w[:, cols].reshape(K // 128, 128, len(cols)).transpose(1, 0, 2))

    wgu = np.empty((L * 2 * FC, 128, 2, DC, 128), f32)
    wd = np.empty((L * 2 * DC, 128, FC, 128), f32)
    win = np.empty((L * 9, 128, 2, DC, 128), f32)
    wout = np.empty((L * (DC // 2), 128, 2, 8, 128), f32)
    ucols = [np.arange(i * 128, (i + 1) * 128) for i in range(4)]
    ucols += [np.arange(512, 640), np.arange(640, 768)]
    for j in range(4):
        ucols += [np.arange(1280 + j * 128, 1280 + (j + 1) * 128),
                  np.arange(1792 + j * 128, 1792 + (j + 1) * 128),
                  np.arange(768 + j * 128, 768 + (j + 1) * 128)]
    for l in range(L):
        for n, (wg, wu, wdn) in enumerate(((ffn1_wg, ffn1_wu, ffn1_wd), (ffn2_wg, ffn2_wu, ffn2_wd))):
            g3 = wg[l].reshape(DC, 128, FC, 128).transpose(2, 1, 0, 3)
            u3 = wu[l].reshape(DC, 128, FC, 128).transpose(2, 1, 0, 3)
            base = (l * 2 + n) * FC
            wgu[base:base + FC, :, 0] = g3
            wgu[base:base + FC, :, 1] = u3
            d3 = wdn[l].reshape(FC, 128, DC, 128).transpose(2, 1, 0, 3)
            wd[(l * 2 + n) * DC:(l * 2 + n + 1) * DC] = d3
        for ui, cols in enumerate(ucols):
            win[l * 9 + ui // 2, :, ui % 2] = kxm(w_in[l], cols)
        for dc in range(DC):
            wout[l * (DC // 2) + dc // 2, :, dc % 2] = kxm(w_out[l], np.arange(dc * 128, (dc + 1) * 128))
    gl = []
    for l in range(L):
        gl += [ffn1_norm[l], mix_norm[l], ffn2_norm[l]]
    gl.append(final_norm)
    gains = np.ascontiguousarray(np.stack(gl).reshape(len(gl), DC, 128).transpose(2, 0, 1).reshape(128, -1)).astype(f32)
    convw = np.ascontiguousarray(conv_w.reshape(L, 3, 4, 128).transpose(3, 0, 1, 2).reshape(128, -1)).astype(f32)
    sink = np.ascontiguousarray(attn_sink.reshape(1, -1)).astype(f32)
    return dict(wgu=wgu.reshape(L * 2 * FC, 128, -1), wd=wd.reshape(L * 2 * DC, 128, -1),
                win=win.reshape(L * 9, 128, -1), wout=wout.reshape(L * (DC // 2), 128, -1),
                gains=gains, convw=convw, sink=sink)


def const_inputs(first):
    j = np.arange(128)[:, None]
    i = np.arange(128)[None, :]
    cur = (j <= i).astype(np.float32)
    prev = (j > i).astype(np.float32)
    pf = np.zeros_like(prev) if first else prev
    masks = np.concatenate([cur, prev, pf], axis=1)
    slopes = np.exp2(-8.0 * np.arange(1, NQ + 1, dtype=np.float32) / NQ).astype(np.float32)
    tq = np.arange(128, dtype=np.float32)
    qaug = np.zeros((3, 8, 128), np.float32)
    qaug[0] = -8.0 * slopes[:, None] * tq[None, :]
    qaug[1] = 8.0 * slopes[:, None]
    qaug[2] = -8.0 * slopes[:, None] * 128.0
    kaug = np.ones((3, 128), np.float32)
    kaug[1] = tq
    return dict(masks=np.ascontiguousarray(masks), qaug=qaug.reshape(3, -1), kaug=kaug)


_CACHE = {}


def kernel(x, ffn1_norm, ffn1_wg, ffn1_wu, ffn1_wd, mix_norm, w_in, conv_w, attn_sink, w_out,
           ffn2_norm, ffn2_wg, ffn2_wu, ffn2_wd, final_norm):
    x = np.asarray(x)
    B, S, D = x.shape
    cfg = Cfg()
    n_cores = 8
    halves = 2
    NT = cfg.NTOK
    w = prep_weights(cfg, *[np.asarray(a) for a in (
        ffn1_norm, ffn1_wg, ffn1_wu, ffn1_wd, mix_norm, w_in, conv_w, attn_sink, w_out,
        ffn2_norm, ffn2_wg, ffn2_wu, ffn2_wd, final_norm)])
    in_maps = []
    for core in range(n_cores):
        b, hf = core // halves, core % halves
        lo = 0 if hf == 0 else S - NT
        m = dict(w)
        m.update(const_inputs(hf == 0))
        m["xT"] = np.ascontiguousarray(x[b, lo:lo + NT].T)
        in_maps.append(m)
    if "nc" not in _CACHE:
        _CACHE["nc"] = build_program(cfg)[0]
    res = run_bass_kernel_spmd(_CACHE["nc"], in_maps, core_ids=list(range(n_cores)))
    out = np.empty((B, S, D), np.float32)
    for core in range(n_cores):
        b, hf = core // halves, core % halves
        yT = res.results[core]["yT"]
        if hf == 0:
            out[b, 0:NT] = yT.T
        else:
            out[b, NT:S] = yT[:, NT - (S - NT):].T
    return out
```

```python
import numpy as np
import concourse.bass as bass
import concourse.mybir as mybir
from concourse.bass_utils import run_bass_kernel_spmd

F32 = mybir.dt.float32
BF16 = mybir.dt.bfloat16
AF = mybir.ActivationFunctionType
ALU = mybir.AluOpType

NQ, NKV, HD = 8, 2, 64
EPS = 1e-6
HALO = 2


class Cfg:
    def __init__(self, D=1024, F=2816, L=2, chunks=(9, 8, 8, 8), NA=4, NB=3, pause=True):
        self.D, self.F, self.L = D, F, L
        self.DC, self.FC = D // 128, F // 128
        self.chunks = list(chunks)
        self.NBLK = sum(chunks)
        self.NTOK = self.NBLK * 128
        self.NOWN = (self.NBLK - HALO) * 128
        self.CMAXB = max(chunks)
        self.CMAX = self.CMAXB * 128
        self.NA, self.NB = NA, NB
        self.AS = max(self.FC, 22)
        self.NG = 3 * L + 1
        self.pause = pause
        self.pause_heads = 4


def tiles_of(nb):
    if nb <= 4:
        return [nb]
    n = -(-nb // 4)
    base, rem = nb // n, nb % n
    return [base + (1 if i < rem else 0) for i in range(n)]


class Tracker:
    ENGS = ("pe", "act", "dve", "pool", "sp")

    def __init__(self):
        self.ops = []
        self.eng_ops = {e: [] for e in self.ENGS}
        self.last_w = {}
        self.readers = {}
        self.dma_cnt = {}

    def add(self, eng, fn, reads=(), writes=(), dma=None):
        oid = len(self.ops)
        raw, other = set(), set()
        for k in reads:
            w = self.last_w.get(k)
            if w is not None:
                raw.add(w)
        for k in writes:
            w = self.last_w.get(k)
            if w is not None:
                other.add(w)
            rd = self.readers.get(k)
            if rd:
                other.update(rd.values())
        op = dict(id=oid, eng=eng, fn=fn, raw=raw, deps=raw | other, dma=dma,
                  seq=len(self.eng_ops[eng]), inc=False, val=None)
        if dma is not None:
            self.dma_cnt[dma] = self.dma_cnt.get(dma, 0) + 16
            op["val"] = self.dma_cnt[dma]
        self.ops.append(op)
        self.eng_ops[eng].append(op)
        for k in writes:
            self.last_w[k] = oid
            self.readers[k] = {}
        rk = eng if dma is None else ("dma", oid)
        for k in reads:
            self.readers.setdefault(k, {})[rk] = oid
        return oid

    def finalize(self):
        for op in self.ops:
            waits = []
            for d in op["deps"]:
                p = self.ops[d]
                if p["dma"] is not None:
                    waits.append(p)
                    continue
                if p["eng"] != op["eng"]:
                    p["inc"] = True
                    waits.append(p)
                else:
                    if op["dma"] is not None:
                        p["inc"] = True
                        waits.append(p)
                    elif op["eng"] in ("act", "dve", "pool") and d in op["raw"] and op["seq"] - p["seq"] <= 2:
                        p["inc"] = True
                        waits.append(p)
            op["waits"] = waits
        cnt = {e: 0 for e in self.ENGS}
        for op in self.ops:
            if op["dma"] is None and op["inc"]:
                cnt[op["eng"]] += 1
                op["val"] = cnt[op["eng"]]

    def emit(self, eng, engobj, sems, dma_sems):
        known = {}
        n_wait = 0
        for op in self.eng_ops[eng]:
            need = {}
            for p in op["waits"]:
                if p["dma"] is not None:
                    s = ("dma", p["dma"])
                else:
                    s = ("eng", p["eng"])
                if p["val"] > need.get(s, 0):
                    need[s] = p["val"]
            for s, v in need.items():
                if known.get(s, 0) >= v:
                    continue
                known[s] = v
                sem = dma_sems[s[1]] if s[0] == "dma" else sems[s[1]]
                engobj.wait_ge(sem, v)
                n_wait += 1
            ins = op["fn"](engobj)
            if op["dma"] is not None:
                ins.then_inc(dma_sems[op["dma"]], 16)
            elif op["inc"]:
                ins.then_inc(sems[eng], 1)
        return n_wait


def build_program(cfg):
    D, F, L, DC, FC = cfg.D, cfg.F, cfg.L, cfg.DC, cfg.FC
    NA, NB, AS, CMAX, CMAXB = cfg.NA, cfg.NB, cfg.AS, cfg.CMAX, cfg.CMAXB
    nc = bass.Bass("TRN2", target_bir_lowering=False)
    T = Tracker()

    d_x = nc.dram_tensor("xT", [D, cfg.NTOK], F32, kind="ExternalInput").ap()
    d_y = nc.dram_tensor("yT", [D, cfg.NTOK], F32, kind="ExternalOutput").ap()
    d_wgu = nc.dram_tensor("wgu", [L * 2 * FC, 128, 2 * DC * 128], F32, kind="ExternalInput").ap()
    d_wd = nc.dram_tensor("wd", [L * 2 * DC, 128, FC * 128], F32, kind="ExternalInput").ap()
    d_win = nc.dram_tensor("win", [L * 9, 128, 2 * DC * 128], F32, kind="ExternalInput").ap()
    d_wout = nc.dram_tensor("wout", [L * (DC // 2), 128, 2 * 8 * 128], F32, kind="ExternalInput").ap()
    d_gain = nc.dram_tensor("gains", [128, cfg.NG * DC], F32, kind="ExternalInput").ap()
    d_convw = nc.dram_tensor("convw", [128, L * 12], F32, kind="ExternalInput").ap()
    d_sink = nc.dram_tensor("sink", [1, L * 8], F32, kind="ExternalInput").ap()
    d_masks = nc.dram_tensor("masks", [128, 3 * 128], F32, kind="ExternalInput").ap()
    d_qaug = nc.dram_tensor("qaug", [3, 8 * 128], F32, kind="ExternalInput").ap()
    d_kaug = nc.dram_tensor("kaug", [3, 128], F32, kind="ExternalInput").ap()

    import contextlib
    es = contextlib.ExitStack()
    with es:
        def sb(name, shape, dt):
            return es.enter_context(nc.sbuf_tensor(name, shape, dt))

        xs = sb("xs", [128, DC, CMAX], F32)
        hT = sb("hT", [128, DC, CMAX], BF16)
        aT = sb("aT", [128, AS, CMAX], BF16)
        wA = [sb(f"wA{i}", [128, 2, 8, 128], BF16) for i in range(NA)]
        wB = [sb(f"wB{i}", [128, FC, 128], BF16) for i in range(NB)]
        sgt = [sb(f"sg{i}", [128, 512], F32) for i in range(2)]
        sqt = [sb(f"sq{i}", [128, 512], BF16) for i in range(2 * DC)]
        rstd = [sb(f"rstd{i}", [128, 512], F32) for i in range(2)]
        ones_bf = sb("ones_bf", [128, 128], BF16)
        gains = sb("gains_sb", [128, cfg.NG * DC], F32)
        convw = sb("convw_sb", [128, L * 12], F32)
        sink_sb = sb("sink_sb", [1, L * 8], F32)
        sinkexp = sb("sinkexp", [1, L * 8], F32)
        sinkrow = sb("sinkrow", [1, L * 8, 128], BF16)
        sel = sb("sel", [1, 2, 128], BF16)
        masks = sb("masks_sb", [128, 3, 128], BF16)
        qaug_c = sb("qaug_c", [128, 8, 128], BF16)
        kaug_c = sb("kaug_c", [128, 128], BF16)
        KTc = [sb(f"KTc{l}", [128, 2, 128], BF16) for l in range(L)]
        Vc = [sb(f"Vc{l}", [128, 2, 192], BF16) for l in range(L)]
        Vb = sb("Vb", [128, CMAXB, 2, 192], BF16)
        ub = sb("ub", [128, 2 + CMAX], F32)
        ucarry = sb("ucarry", [128, L * 4, 2], F32)
        PT = [sb(f"PT{i}", [128, 512], BF16) for i in range(6)]
        rscr = [sb(f"rscr{i}", [128, 512], F32) for i in range(2)]
        den = [sb(f"den{i}", [128, 512], F32) for i in range(2)]
        rec = [sb(f"rec{i}", [128, 512], F32) for i in range(2)]
        ps = [es.enter_context(nc.psum_tensor(f"ps{i}", [128, 512], F32)) for i in range(8)]

        tC_full = aT[:, 18:20, :].rearrange("p a c -> p (a c)").bitcast(F32)
        yb_full = aT[:, 20:22, :].rearrange("p a c -> p (a c)").bitcast(F32)

        sems = {e: es.enter_context(nc.semaphore(f"s_{e}")) for e in Tracker.ENGS}
        dma_keys = ([("wA", i) for i in range(NA)] + [("wB", i) for i in range(NB)] +
                    [("xld", i) for i in range(4)] + [("st", i) for i in range(4)] + [("cst", i) for i in range(6)])
        dma_sems = {k: es.enter_context(nc.semaphore("d_%s%d" % k)) for k in dma_keys}

        ring_ctr = {}

        def ring(name, n):
            i = ring_ctr.get(name, 0)
            ring_ctr[name] = i + 1
            return i % n

        wlistA, wlistB = [], []
        issuedA, issuedB = [0], [0]

        def plan_weights():
            for _ci in range(len(cfg.chunks)):
                for l in range(L):
                    for n in range(2):
                        if n == 1:
                            for p_ in range(9):
                                wlistA.append((d_win[l * 9 + p_], DC))
                            for p_ in range(DC // 2):
                                wlistA.append((d_wout[l * (DC // 2) + p_], 8))
                        for fc in range(FC):
                            wlistA.append((d_wgu[(l * 2 + n) * FC + fc], DC))
                        for dc in range(DC):
                            wlistB.append(d_wd[(l * 2 + n) * DC + dc])
        plan_weights()
        useA, useB = [0], [0]

        def issue_A(upto):
            while issuedA[0] <= upto and issuedA[0] < len(wlistA):
                i = issuedA[0]
                issuedA[0] += 1
                src, kc = wlistA[i]
                slot = i % NA
                dst = wA[slot][:, :, 0:kc, :]
                srcv = src.rearrange("p (s c f) -> p s c f", s=2, c=kc)
                T.add("pool", lambda e, dst=dst, srcv=srcv: e.dma_start(out=dst, in_=srcv),
                      writes=[("wA", slot)], dma=("wA", slot))

        def issue_B(upto):
            while issuedB[0] <= upto and issuedB[0] < len(wlistB):
                i = issuedB[0]
                issuedB[0] += 1
                src = wlistB[i]
                slot = i % NB
                dst = wB[slot][:, :, :]
                srcv = src.rearrange("p (k f) -> p k f", k=FC)
                T.add("pool", lambda e, dst=dst, srcv=srcv: e.dma_start(out=dst, in_=srcv),
                      writes=[("wB", slot)], dma=("wB", slot))

        def next_A(hold=0):
            i = useA[0]
            useA[0] += 1
            issue_A(i + NA - 1 - hold)
            issue_B(useB[0] + NB - 2)
            return i % NA

        def next_B():
            i = useB[0]
            useB[0] += 1
            issue_B(i + NB - 1)
            issue_A(useA[0] + NA - 2)
            return i % NB

        T.add("sp", lambda e: e.dma_start(out=gains[:, :], in_=d_gain[:, :]), writes=[("gains",)], dma=("cst", 0))
        T.add("sp", lambda e: e.dma_start(out=convw[:, :], in_=d_convw[:, :]), writes=[("convw",)], dma=("cst", 1))
        T.add("sp", lambda e: e.dma_start(out=sink_sb[:, :], in_=d_sink[:, :]), writes=[("sink",)], dma=("cst", 2))
        T.add("pool", lambda e: e.dma_start(out=masks[:, :, :], in_=d_masks.rearrange("p (m k) -> p m k", m=3)),
              writes=[("masks",)], dma=("cst", 3))
        T.add("pool", lambda e: e.dma_start(out=qaug_c[64:67, :, :], in_=d_qaug.rearrange("p (h k) -> p h k", h=8)),
              writes=[("qaug",)], dma=("cst", 4))
        T.add("pool", lambda e: e.dma_start(out=kaug_c[64:67, :], in_=d_kaug[:, :]),
              writes=[("kaug",)], dma=("cst", 5))
        T.add("dve", lambda e: e.memset(ones_bf[:, :], 1.0), writes=[("ones",)])
        T.add("dve", lambda e: e.memset(sel[:, :, :], 1.0), writes=[("sel",)])
        T.add("dve", lambda e: e.memset(sel[0:1, 0, 0:64], 0.0), writes=[("sel",)])
        T.add("dve", lambda e: e.memset(sel[0:1, 1, 64:128], 0.0), writes=[("sel",)])
        T.add("dve", lambda e: e.memset(ucarry[:, :, :], 0.0), writes=[("ucarry", i) for i in range(L * 4)])
        for l in range(L):
            T.add("dve", lambda e, l=l: e.memset(KTc[l][:, :, :], 0.0), writes=[("KTc", l)])
            T.add("dve", lambda e, l=l: e.memset(Vc[l][:, :, :], 1.0), writes=[("Vc", l)])
            T.add("dve", lambda e, l=l: e.memset(Vc[l][:, :, 64:128], 0.0), writes=[("Vc", l)])
            T.add("pool", lambda e, l=l: e.tensor_copy(
                out=KTc[l][64:67, :, :], in_=kaug_c[64:67, :].unsqueeze(1).broadcast_to([3, 2, 128])),
                reads=[("kaug",)], writes=[("KTc", l)])
        T.add("dve", lambda e: e.memset(Vb[:, :, :, :], 1.0), writes=[("Vb", b) for b in range(CMAXB)])
        T.add("act", lambda e: e.activation(out=sinkexp[:, :], in_=sink_sb[:, :], func=AF.Exp),
              reads=[("sink",)], writes=[("sinkexp",)])
        T.add("dve", lambda e: e.tensor_copy(
            out=sinkrow[:, :, :], in_=sinkexp[:, :].unsqueeze(2).broadcast_to([1, L * 8, 128])),
            reads=[("sinkexp",)], writes=[("sinkrow",)])

        def kx(c, tl):
            return [("x", c, b) for b in range(tl[0], tl[0] + tl[1])]

        def kh(tl):
            return [("h", b) for b in range(tl[0], tl[0] + tl[1])]

        def ka(slot, tl):
            return [("a", slot, b) for b in range(tl[0], tl[0] + tl[1])]

        def tok(tl):
            return slice(tl[0] * 128, (tl[0] + tl[1]) * 128)

        def recip(out_ap, in_ap, scr_ap, reads, wkey, skey, scale=-1.0, in_scale=1.0, in_bias=0.0):
            T.add("act", lambda e: e.activation(out=scr_ap, in_=in_ap, func=AF.Ln, bias=in_bias, scale=in_scale),
                  reads=reads, writes=[skey])
            T.add("act", lambda e: e.activation(out=out_ap, in_=scr_ap, func=AF.Exp, scale=scale),
                  reads=[skey], writes=[wkey])

        NSQ = 2 * DC
        deferred = []

        def flush_deferred():
            while deferred:
                deferred.pop(0)()

        def norm_A(tl):
            N = tl[1] * 128
            ts = tok(tl)
            slots = []
            for c in range(DC):
                r = ring("sq", NSQ)
                slots.append(r)
                T.add("act", lambda e, r=r, c=c, ts=ts, N=N: e.activation(
                    out=sqt[r][:, 0:N], in_=xs[:, c, ts], func=AF.Square),
                    reads=kx(c, tl), writes=[("sq", r)])
            return slots

        def norm_B(tl, slots, gi, final):
            N = tl[1] * 128
            ts = tok(tl)
            pb = 6 + ring("psn", 2)
            for c in range(DC):
                r = slots[c]
                T.add("pe", lambda e, r=r, c=c, N=N, pb=pb: e.matmul(
                    ps[pb][:, 0:N], ones_bf[:, :], sqt[r][:, 0:N], start=(c == 0), stop=(c == DC - 1)),
                    reads=[("sq", r), ("ones",)], writes=[("ps", pb)])
            rr = ring("rstd", 2)
            recip(rstd[rr][:, 0:N], ps[pb][:, 0:N], rscr[rr][:, 0:N], [("ps", pb)],
                  ("rstd", rr), ("rscr", rr), scale=-0.5, in_scale=1.0 / D, in_bias=EPS)
            for c in range(DC):
                gcol = gains[:, gi * DC + c: gi * DC + c + 1]
                if final:
                    T.add("dve", lambda e, c=c, ts=ts, N=N, rr=rr, gcol=gcol: e.scalar_tensor_tensor(
                        out=xs[:, c, ts], in0=xs[:, c, ts], scalar=gcol, in1=rstd[rr][:, 0:N],
                        op0=ALU.mult, op1=ALU.mult),
                        reads=kx(c, tl) + [("rstd", rr), ("gains",)], writes=kx(c, tl))
                else:
                    T.add("dve", lambda e, c=c, ts=ts, N=N, rr=rr, gcol=gcol: e.scalar_tensor_tensor(
                        out=hT[:, c, ts], in0=xs[:, c, ts], scalar=gcol, in1=rstd[rr][:, 0:N],
                        op0=ALU.mult, op1=ALU.mult),
                        reads=kx(c, tl) + [("rstd", rr), ("gains",)], writes=kh(tl))

        def norm_plain(tiles, gi, final=False):
            for tl in tiles:
                norm_B(tl, norm_A(tl), gi, final)

        saved_sq = {}

        def last_stage_hook(i, tiles, nxt):
            gi, final = nxt
            if i > 0:
                norm_B(tiles[i - 1], saved_sq[i - 1], gi, final)
            saved_sq[i] = norm_A(tiles[i])
            if i == len(tiles) - 1:
                tl_, sl_ = tiles[i], saved_sq[i]
                if final or len(tiles) == 1:
                    norm_B(tl_, sl_, gi, final)
                else:
                    deferred.append(lambda: norm_B(tl_, sl_, gi, final))

        def ffn(tiles, l, n, nxt):
            order = []
            if len(tiles) >= 2 and FC >= 2:
                for t_ in tiles[:-1]:
                    order.append((0, t_))
                for t_ in tiles[:-1]:
                    order.append((1, t_))
                order += [(0, tiles[-1]), (1, tiles[-1])]
                order += [(fc, t_) for fc in range(2, FC) for t_ in tiles]
            else:
                order = [(fc, t_) for fc in range(FC) for t_ in tiles]
            fslots = {}
            for fc, tl in order:
                if fc not in fslots:
                    fslots[fc] = next_A(hold=1 if (fc == 1 and len(tiles) >= 2 and FC >= 2) else 0)
                slot = fslots[fc]
                if True:
                    N = tl[1] * 128
                    ts = tok(tl)
                    gb = 0 + ring("psg", 2)
                    ubk = 2 + ring("psu", 2)
                    for s, bank in ((0, gb), (1, ubk)):
                        for c in range(DC):
                            T.add("pe", lambda e, s=s, c=c, bank=bank, slot=slot, ts=ts, N=N: e.matmul(
                                ps[bank][:, 0:N], wA[slot][:, s, c, :], hT[:, c, ts],
                                start=(c == 0), stop=(c == DC - 1)),
                                reads=[("wA", slot)] + kh(tl), writes=[("ps", bank)])
                    r = ring("sg", 2)
                    T.add("act", lambda e, r=r, gb=gb, N=N: e.activation(
                        out=sgt[r][:, 0:N], in_=ps[gb][:, 0:N], func=AF.Silu),
                        reads=[("ps", gb)], writes=[("sg", r)])
                    T.add("dve", lambda e, r=r, ubk=ubk, N=N, fc=fc, ts=ts: e.tensor_tensor(
                        out=aT[:, fc, ts], in0=ps[ubk][:, 0:N], in1=sgt[r][:, 0:N], op=ALU.mult),
                        reads=[("ps", ubk), ("sg", r)], writes=ka(fc, tl))
                    flush_deferred()
            for dc in range(DC):
                slot = next_B()
                for tl in tiles:
                    N = tl[1] * 128
                    ts = tok(tl)
                    yb_ = 4 + ring("psy", 2)
                    for k in range(FC):
                        T.add("pe", lambda e, k=k, yb_=yb_, slot=slot, ts=ts, N=N: e.matmul(
                            ps[yb_][:, 0:N], wB[slot][:, k, :], aT[:, k, ts],
                            start=(k == 0), stop=(k == FC - 1)),
                            reads=[("wB", slot)] + ka(k, tl), writes=[("ps", yb_)])
                    T.add("dve", lambda e, dc=dc, yb_=yb_, ts=ts, N=N: e.scalar_tensor_tensor(
                        out=xs[:, dc, ts], in0=ps[yb_][:, 0:N], scalar=0.5, in1=xs[:, dc, ts],
                        op0=ALU.mult, op1=ALU.add),
                        reads=[("ps", yb_)] + kx(dc, tl), writes=kx(dc, tl))
                    if dc == DC - 1:
                        last_stage_hook(tiles.index(tl), tiles, nxt)

        def mix(ci, nb, tiles, l, nxt):
            C = nb * 128
            allt = (0, nb)
            flush_deferred()
            PH = cfg.pause_heads
            if cfg.pause:
                T.add("pool", lambda e: e.tensor_copy(
                    out=aT[64:67, 8:8 + PH, 0:C].rearrange("p h (b t) -> p h b t", t=128),
                    in_=qaug_c[64:67, 0:PH, :].unsqueeze(2).broadcast_to([3, PH, nb, 128])),
                    reads=[("qaug",)] + kx(DC - 1, tiles[-1]),
                    writes=[("pause",)] + [k for h in range(PH) for k in ka(8 + h, allt)])
            for tl in tiles:
                ts = tok(tl)
                T.add("act", lambda e, ts=ts, tl=tl: e.activation(
                    out=aT[64:67, 16:18, ts].rearrange("p h (b t) -> p h b t", t=128),
                    in_=kaug_c[64:67, :].unsqueeze(1).unsqueeze(1).broadcast_to([3, 2, tl[1], 128]),
                    func=AF.Copy),
                    reads=[("kaug",)], writes=[k for h in (16, 17) for k in ka(h, tl)])
                T.add("act", lambda e, ts=ts, tl=tl: e.activation(
                    out=aT[64:67, 0:8, ts].rearrange("p h (b t) -> p h b t", t=128),
                    in_=qaug_c[64:67, :, :].unsqueeze(2).broadcast_to([3, 8, tl[1], 128]),
                    func=AF.Copy),
                    reads=[("qaug",)], writes=[k for h in range(8) for k in ka(h, tl)])

            unit_names = ["q0", "q1", "q2", "q3", "k", "v"]
            for j in range(4):
                unit_names += [f"C{j}", f"H{j}", f"B{j}"]
            slot = None
            for ui, un in enumerate(unit_names):
                s = ui % 2
                if s == 0:
                    slot = next_A()
                if un == "v":
                    for tl in tiles:
                        bank = ring("psp", 4)
                        for jb in range(tl[1]):
                            b = tl[0] + jb
                            for c in range(DC):
                                T.add("pe", lambda e, s=s, c=c, bank=bank, slot=slot, jb=jb, b=b: e.matmul(
                                    ps[bank][:, jb * 128:(jb + 1) * 128], hT[:, c, b * 128:(b + 1) * 128],
                                    wA[slot][:, s, c, :], start=(c == 0), stop=(c == DC - 1)),
                                    reads=[("wA", slot), ("h", b)], writes=[("ps", bank)])
                        T.add("act", lambda e, bank=bank, tl=tl: e.activation(
                            out=Vb[:, tl[0]:tl[0] + tl[1], :, 64:128],
                            in_=ps[bank][:, 0:tl[1] * 128].rearrange("p (b g d) -> p b g d", g=2, d=64),
                            func=AF.Copy),
                            reads=[("ps", bank)], writes=[("Vb", b) for b in range(tl[0], tl[0] + tl[1])])
                    continue
                for tl in tiles:
                    N = tl[1] * 128
                    ts = tok(tl)
                    bank = ring("psp", 4)
                    for c in range(DC):
                        T.add("pe", lambda e, s=s, c=c, bank=bank, slot=slot, ts=ts, N=N: e.matmul(
                            ps[bank][:, 0:N], wA[slot][:, s, c, :], hT[:, c, ts],
                            start=(c == 0), stop=(c == DC - 1)),
                            reads=[("wA", slot)] + kh(tl) + ([("pause",)] if ui == 0 else []), writes=[("ps", bank)])
                    if un[0] == "q":
                        i = int(un[1])
                        T.add("act", lambda e, bank=bank, i=i, ts=ts, N=N: e.activation(
                            out=aT[0:64, 2 * i, ts], in_=ps[bank][0:64, 0:N], func=AF.Copy),
                            reads=[("ps", bank)], writes=ka(2 * i, tl))
                        T.add("dve", lambda e, bank=bank, i=i, ts=ts, N=N: e.tensor_copy(
                            out=aT[0:64, 2 * i + 1, ts], in_=ps[bank][64:128, 0:N]),
                            reads=[("ps", bank)], writes=ka(2 * i + 1, tl))
                    elif un == "k":
                        T.add("act", lambda e, bank=bank, ts=ts, N=N: e.activation(
                            out=aT[0:64, 16, ts], in_=ps[bank][0:64, 0:N], func=AF.Copy),
                            reads=[("ps", bank)], writes=ka(16, tl))
                        T.add("dve", lambda e, bank=bank, ts=ts, N=N: e.tensor_copy(
                            out=aT[0:64, 17, ts], in_=ps[bank][64:128, 0:N]),
                            reads=[("ps", bank)], writes=ka(17, tl))
                    elif un[0] == "C":
                        T.add("act", lambda e, bank=bank, ts=ts, N=N: e.activation(
                            out=tC_full[:, ts], in_=ps[bank][:, 0:N], func=AF.Copy),
                            reads=[("ps", bank)], writes=ka(18, tl) + ka(19, tl))
                    elif un[0] == "H":
                        j = int(un[1])
                        T.add("dve", lambda e, bank=bank, ts=ts, N=N, tl=tl: e.tensor_tensor(
                            out=ub[:, 2 + tl[0] * 128: 2 + (tl[0] + tl[1]) * 128], in0=ps[bank][:, 0:N],
                            in1=tC_full[:, ts], op=ALU.mult),
                            reads=[("ps", bank)] + ka(18, tl) + ka(19, tl),
                            writes=[("ub", b) for b in range(tl[0], tl[0] + tl[1])])
                        if tl is tiles[-1]:
                            ubk_all = [("ub", b) for b in range(nb)]
                            ykeys = ka(20, allt) + ka(21, allt)
                            wbase = (l * 3) * 4 + j
                            T.add("pool", lambda e, l=l, j=j: e.tensor_copy(
                                out=ub[:, 0:2], in_=ucarry[:, l * 4 + j, :]),
                                reads=[("ucarry", l * 4 + j)], writes=[("ub", "c")])
                            T.add("dve", lambda e, wbase=wbase: e.tensor_scalar(
                                out=yb_full[:, 0:C], in0=ub[:, 2:2 + C], scalar1=convw[:, wbase + 8: wbase + 9],
                                scalar2=None, op0=ALU.mult),
                                reads=ubk_all + [("convw",)], writes=ykeys)
                            T.add("dve", lambda e, wbase=wbase: e.scalar_tensor_tensor(
                                out=yb_full[:, 0:C], in0=ub[:, 1:1 + C], scalar=convw[:, wbase + 4: wbase + 5],
                                in1=yb_full[:, 0:C], op0=ALU.mult, op1=ALU.add),
                                reads=ubk_all + [("ub", "c"), ("convw",)] + ykeys, writes=ykeys)
                            T.add("dve", lambda e, wbase=wbase: e.scalar_tensor_tensor(
                                out=yb_full[:, 0:C], in0=ub[:, 0:C], scalar=convw[:, wbase: wbase + 1],
                                in1=yb_full[:, 0:C], op0=ALU.mult, op1=ALU.add),
                                reads=ubk_all + [("ub", "c"), ("convw",)] + ykeys, writes=ykeys)
                            T.add("pool", lambda e, l=l, j=j: e.tensor_copy(
                                out=ucarry[:, l * 4 + j, :], in_=ub[:, C:C + 2]),
                                reads=ubk_all, writes=[("ucarry", l * 4 + j)])
                    elif un[0] == "B":
                        j = int(un[1])
                        T.add("dve", lambda e, bank=bank, ts=ts, N=N, j=j: e.tensor_tensor(
                            out=aT[:, 12 + j, ts], in0=ps[bank][:, 0:N], in1=yb_full[:, ts], op=ALU.mult),
                            reads=[("ps", bank)] + ka(20, tl) + ka(21, tl), writes=ka(12 + j, tl))
                    flush_deferred()

            items = [(b, g) for b in range(nb) for g in range(2)]
            issue_A(useA[0] + NA - 2)

            def emit_scores(b, g):
                tl1 = (b, 1)
                res = []
                for kb in (0, 1):
                    nr = 67 if kb == 0 else 66
                    if kb == 1:
                        kap = aT[0:nr, 16 + g, b * 128:(b + 1) * 128]
                        kkeys = [("a", 16 + g, b)]
                    elif b == 0:
                        kap = KTc[l][0:nr, g, :]
                        kkeys = [("KTc", l)]
                    else:
                        kap = aT[0:nr, 16 + g, (b - 1) * 128: b * 128]
                        kkeys = [("a", 16 + g, b - 1)]
                    bank = ring("pss", 6)
                    qap = aT[0:nr, 4 * g:4 * g + 4, b * 128:(b + 1) * 128]
                    T.add("pe", lambda e, bank=bank, kap=kap, qap=qap: e.matmul(
                        ps[bank][:, :].rearrange("p (h t) -> p h t", t=128), kap, qap, start=True, stop=True),
                        reads=kkeys + [("a", 4 * g + hh, b) for hh in range(4)], writes=[("ps", bank)])
                    pr = ring("PT", 6)
                    T.add("act", lambda e, bank=bank, pr=pr: e.activation(
                        out=PT[pr][:, :], in_=ps[bank][:, :], func=AF.Exp, scale=0.125),
                        reads=[("ps", bank)], writes=[("PT", pr)])
                    if kb == 1:
                        mi = 0
                    else:
                        mi = 2 if (ci == 0 and b == 0) else 1
                    T.add("pool", lambda e, pr=pr, mi=mi: e.tensor_tensor(
                        out=PT[pr][:, :].rearrange("p (h t) -> p h t", t=128),
                        in0=PT[pr][:, :].rearrange("p (h t) -> p h t", t=128),
                        in1=masks[:, mi, :].unsqueeze(1).broadcast_to([128, 4, 128]), op=ALU.mult),
                        reads=[("PT", pr), ("masks",)], writes=[("PT", pr)])
                    res.append(pr)
                return res

            def emit_pv(b, g, prs):
                ob = 6 + ring("pso", 2)
                vaps, vkeys = [], []
                for kb in (0, 1):
                    if kb == 1:
                        vaps.append(Vb[:, b, g, :]); vkeys.append(("Vb", b))
                    elif b == 0:
                        vaps.append(Vc[l][:, g, :]); vkeys.append(("Vc", l))
                    else:
                        vaps.append(Vb[:, b - 1, g, :]); vkeys.append(("Vb", b - 1))
                for par in (0, 1):
                    osl = slice(par * 256, (par + 1) * 256)
                    for kb in (0, 1):
                        lhs = vaps[kb][:, 64:192] if par == 0 else vaps[kb][:, 0:128]
                        rhs = PT[prs[kb]][:, :].rearrange("p (h t) -> p h t", t=128)[:, par:4:2, :]
                        T.add("pe", lambda e, ob=ob, osl=osl, lhs=lhs, rhs=rhs, kb=kb: e.matmul(
                            ps[ob][:, osl].rearrange("p (h t) -> p h t", t=128), lhs, rhs,
                            start=(kb == 0), stop=False),
                            reads=[vkeys[kb], ("PT", prs[kb])], writes=[("ps", ob)])
                    h0 = l * 8 + 4 * g + par
                    T.add("pe", lambda e, ob=ob, osl=osl, par=par, h0=h0: e.matmul(
                        ps[ob][:, osl].rearrange("p (h t) -> p h t", t=128), sel[0:1, par, :],
                        sinkrow[0:1, h0:h0 + 3:2, :], start=False, stop=True),
                        reads=[("sel",), ("sinkrow",)], writes=[("ps", ob)])
                rr = ring("rec", 2)
                T.add("dve", lambda e, ob=ob, rr=rr: e.tensor_copy(
                    out=den[rr][0:64, 0:256], in_=ps[ob][64:128, 0:256]),
                    reads=[("ps", ob)], writes=[("den", rr, 0)])
                T.add("dve", lambda e, ob=ob, rr=rr: e.tensor_copy(
                    out=den[rr][64:128, 256:512], in_=ps[ob][0:64, 256:512]),
                    reads=[("ps", ob)], writes=[("den", rr, 1)])
                recip(rec[rr][0:64, 0:256], den[rr][0:64, 0:256], rscr[rr][0:64, 0:256],
                      [("den", rr, 0)], ("rec", rr, 0), ("rscr", rr, 0))
                recip(rec[rr][64:128, 256:512], den[rr][64:128, 256:512], rscr[rr][64:128, 256:512],
                      [("den", rr, 1)], ("rec", rr, 1), ("rscr", rr, 1))
                bs = slice(b * 128, (b + 1) * 128)
                T.add("dve", lambda e, ob=ob, rr=rr, bs=bs, g=g: e.tensor_tensor(
                    out=aT[0:64, 8 + 2 * g: 8 + 2 * g + 2, bs],
                    in0=ps[ob][0:64, 0:256].rearrange("p (h t) -> p h t", t=128),
                    in1=rec[rr][0:64, 0:256].rearrange("p (h t) -> p h t", t=128), op=ALU.mult),
                    reads=[("ps", ob), ("rec", rr, 0)],
                    writes=[("a", 8 + 2 * g, b, 0), ("a", 8 + 2 * g + 1, b, 0)])
                T.add("dve", lambda e, ob=ob, rr=rr, bs=bs, g=g: e.tensor_tensor(
                    out=aT[64:128, 8 + 2 * g: 8 + 2 * g + 2, bs],
                    in0=ps[ob][64:128, 256:512].rearrange("p (h t) -> p h t", t=128),
                    in1=rec[rr][64:128, 256:512].rearrange("p (h t) -> p h t", t=128), op=ALU.mult),
                    reads=[("ps", ob), ("rec", rr, 1)],
                    writes=[("a", 8 + 2 * g, b, 1), ("a", 8 + 2 * g + 1, b, 1)])

            pend = [emit_scores(*items[0])]
            if len(items) > 1:
                pend.append(emit_scores(*items[1]))
            for i, (b, g) in enumerate(items):
                if i + 2 < len(items):
                    pend.append(emit_scores(*items[i + 2]))
                emit_pv(b, g, pend.pop(0))
            T.add("act", lambda e: e.activation(
                out=KTc[l][0:64, :, :], in_=aT[0:64, 16:18, (nb - 1) * 128: nb * 128], func=AF.Copy),
                reads=[("a", 16, nb - 1), ("a", 17, nb - 1)], writes=[("KTc", l)])
            T.add("act", lambda e: e.activation(
                out=Vc[l][:, :, 64:128], in_=Vb[:, nb - 1, :, 64:128], func=AF.Copy),
                reads=[("Vb", nb - 1)], writes=[("Vc", l)])

            slot = None
            for dc in range(DC):
                s = dc % 2
                if s == 0:
                    slot = next_A()
                for tl in tiles:
                    N = tl[1] * 128
                    ts = tok(tl)
                    bank = ring("psp", 4)
                    mk = []
                    for c in range(8):
                        if c < 4:
                            mk_c = [("a", 8 + c, b, hf) for b in range(tl[0], tl[0] + tl[1]) for hf in (0, 1)]
                        else:
                            mk_c = ka(8 + c, tl)
                        T.add("pe", lambda e, s=s, c=c, bank=bank, slot=slot, ts=ts, N=N: e.matmul(
                            ps[bank][:, 0:N], wA[slot][:, s, c, :], aT[:, 8 + c, ts],
                            start=(c == 0), stop=(c == 7)),
                            reads=[("wA", slot)] + mk_c, writes=[("ps", bank)])
                    T.add("dve", lambda e, dc=dc, bank=bank, ts=ts, N=N: e.tensor_tensor(
                        out=xs[:, dc, ts], in0=ps[bank][:, 0:N], in1=xs[:, dc, ts], op=ALU.add),
                        reads=[("ps", bank)] + kx(dc, tl), writes=kx(dc, tl))
                    if dc == DC - 1:
                        last_stage_hook(tiles.index(tl), tiles, nxt)

        _ka_orig = ka

        def ka(slot, tl):
            ks = _ka_orig(slot, tl)
            if 8 <= slot < 12:
                ks = ks + [("a", slot, b, hf) for b in range(tl[0], tl[0] + tl[1]) for hf in (0, 1)]
            return ks

        xT_v = d_x.rearrange("(c p) t -> p c t", p=128)
        yT_v = d_y.rearrange("(c p) t -> p c t", p=128)
        tok0 = 0
        for ci, nb in enumerate(cfg.chunks):
            tsz = tiles_of(nb)
            tiles, o = [], 0
            for t_ in tsz:
                tiles.append((o, t_))
                o += t_
            for ti, tl in enumerate(tiles):
                ts = tok(tl)
                g0 = tok0 + tl[0] * 128
                T.add("sp", lambda e, ts=ts, g0=g0, N=tl[1] * 128: e.dma_start(
                    out=xs[:, :, ts], in_=xT_v[:, :, g0:g0 + N]),
                    writes=[k for c in range(DC) for k in kx(c, tl)], dma=("xld", ti))
            norm_plain(tiles, 0)
            for l in range(L):
                ffn(tiles, l, 0, (l * 3 + 1, False))
                mix(ci, nb, tiles, l, (l * 3 + 2, False))
                ffn(tiles, l, 1, ((l + 1) * 3, False) if l + 1 < L else (3 * L, True))
            flush_deferred()
            for ti, tl in enumerate(tiles):
                g0 = tok0 + tl[0] * 128
                g1 = g0 + tl[1] * 128
                ts = tok(tl)
                T.add("sp", lambda e, ts=ts, g0=g0, g1=g1: e.dma_start(
                    out=yT_v[:, :, g0:g1], in_=xs[:, :, ts]),
                    reads=[k for c in range(DC) for k in kx(c, tl)], dma=("st", ti))
            tok0 += nb * 128

        T.finalize()
        final_waits = [(dma_sems[k], v) for k, v in T.dma_cnt.items() if k[0] == "st"]

        with nc.Block() as block:
            @block.tensor
            def _(e):
                T.emit("pe", e, sems, dma_sems)

            @block.scalar
            def _(e):
                T.emit("act", e, sems, dma_sems)

            @block.vector
            def _(e):
                T.emit("dve", e, sems, dma_sems)

            @block.gpsimd
            def _(e):
                T.emit("pool", e, sems, dma_sems)

            @block.sync
            def _(e):
                T.emit("sp", e, sems, dma_sems)
                for s_, v in final_waits:
                    e.wait_ge(s_, v)
    return nc, T


def prep_weights(cfg, ffn1_norm, ffn1_wg, ffn1_wu, ffn1_wd, mix_norm, w_in, conv_w, attn_sink, w_out,
                 ffn2_norm, ffn2_wg, ffn2_wu, ffn2_wd, final_norm):
    D, F, L, DC, FC = cfg.D, cfg.F, cfg.L, cfg.DC, cfg.FC
    f32 = np.float32

    def kxm(w, cols):
        K = w.shape[0]
        return np.ascontiguousarray(w[:, cols].reshape(K // 128, 128, len(cols)).transpose(1, 0, 2))

    wgu = np.empty((L * 2 * FC, 128, 2, DC, 128), f32)
    wd = np.empty((L * 2 * DC, 128, FC, 128), f32)
    win = np.empty((L * 9, 128, 2, DC, 128), f32)
    wout = np.empty((L * (DC // 2), 128, 2, 8, 128), f32)
    ucols = [np.arange(i * 128, (i + 1) * 128) for i in range(4)]
    ucols += [np.arange(512, 640), np.arange(640, 768)]
    for j in range(4):
        ucols += [np.arange(1280 + j * 128, 1280 + (j + 1) * 128),
                  np.arange(1792 + j * 128, 1792 + (j + 1) * 128),
                  np.arange(768 + j * 128, 768 + (j + 1) * 128)]
    for l in range(L):
        for n, (wg, wu, wdn) in enumerate(((ffn1_wg, ffn1_wu, ffn1_wd), (ffn2_wg, ffn2_wu, ffn2_wd))):
            g3 = wg[l].reshape(DC, 128, FC, 128).transpose(2, 1, 0, 3)
            u3 = wu[l].reshape(DC, 128, FC, 128).transpose(2, 1, 0, 3)
            base = (l * 2 + n) * FC
            wgu[base:base + FC, :, 0] = g3
            wgu[base:base + FC, :, 1] = u3
            d3 = wdn[l].reshape(FC, 128, DC, 128).transpose(2, 1, 0, 3)
            wd[(l * 2 + n) * DC:(l * 2 + n + 1) * DC] = d3
        for ui, cols in enumerate(ucols):
            win[l * 9 + ui // 2, :, ui % 2] = kxm(w_in[l], cols)
        for dc in range(DC):
            wout[l * (DC // 2) + dc // 2, :, dc % 2] = kxm(w_out[l], np.arange(dc * 128, (dc + 1) * 128))
    gl = []
    for l in range(L):
        gl += [ffn1_norm[l], mix_norm[l], ffn2_norm[l]]
    gl.append(final_norm)
    gains = np.ascontiguousarray(np.stack(gl).reshape(len(gl), DC, 128).transpose(2, 0, 1).reshape(128, -1)).astype(f32)
    convw = np.ascontiguousarray(conv_w.reshape(L, 3, 4, 128).transpose(3, 0, 1, 2).reshape(128, -1)).astype(f32)
    sink = np.ascontiguousarray(attn_sink.reshape(1, -1)).astype(f32)
    return dict(wgu=wgu.reshape(L * 2 * FC, 128, -1), wd=wd.reshape(L * 2 * DC, 128, -1),
                win=win.reshape(L * 9, 128, -1), wout=wout.reshape(L * (DC // 2), 128, -1),
                gains=gains, convw=convw, sink=sink)


def const_inputs(first):
    j = np.arange(128)[:, None]
    i = np.arange(128)[None, :]
    cur = (j <= i).astype(np.float32)
    prev = (j > i).astype(np.float32)
    pf = np.zeros_like(prev) if first else prev
    masks = np.concatenate([cur, prev, pf], axis=1)
    slopes = np.exp2(-8.0 * np.arange(1, NQ + 1, dtype=np.float32) / NQ).astype(np.float32)
    tq = np.arange(128, dtype=np.float32)
    qaug = np.zeros((3, 8, 128), np.float32)
    qaug[0] = -8.0 * slopes[:, None] * tq[None, :]
    qaug[1] = 8.0 * slopes[:, None]
    qaug[2] = -8.0 * slopes[:, None] * 128.0
    kaug = np.ones((3, 128), np.float32)
    kaug[1] = tq
    return dict(masks=np.ascontiguousarray(masks), qaug=qaug.reshape(3, -1), kaug=kaug)


_CACHE = {}


def kernel(x, ffn1_norm, ffn1_wg, ffn1_wu, ffn1_wd, mix_norm, w_in, conv_w, attn_sink, w_out,
           ffn2_norm, ffn2_wg, ffn2_wu, ffn2_wd, final_norm):
    x = np.asarray(x)
    B, S, D = x.shape
    cfg = Cfg()
    n_cores = 8
    halves = 2
    NT = cfg.NTOK
    w = prep_weights(cfg, *[np.asarray(a) for a in (
        ffn1_norm, ffn1_wg, ffn1_wu, ffn1_wd, mix_norm, w_in, conv_w, attn_sink, w_out,
        ffn2_norm, ffn2_wg, ffn2_wu, ffn2_wd, final_norm)])
    in_maps = []
    for core in range(n_cores):
        b, hf = core // halves, core % halves
        lo = 0 if hf == 0 else S - NT
        m = dict(w)
        m.update(const_inputs(hf == 0))
        m["xT"] = np.ascontiguousarray(x[b, lo:lo + NT].T)
        in_maps.append(m)
    if "nc" not in _CACHE:
        _CACHE["nc"] = build_program(cfg)[0]
    res = run_bass_kernel_spmd(_CACHE["nc"], in_maps, core_ids=list(range(n_cores)))
    out = np.empty((B, S, D), np.float32)
    for core in range(n_cores):
        b, hf = core // halves, core % halves
        yT = res.results[core]["yT"]
        if hf == 0:
            out[b, 0:NT] = yT.T
        else:
            out[b, NT:S] = yT[:, NT - (S - NT):].T
    return out
```
